# Optimizing a Trainium2 kernel written in Bass

```python
import jax, jax.numpy as jnp
from jax import lax
import numpy as np

D_MODEL = 2048
BATCH = 2
SEQ = 16384
DEPTH = 2

CHUNK = 64
N_MIXERS = 2
SB_HEADS = 8
SB_HEAD_DIM = 128
SB_INNER = SB_HEADS * SB_HEAD_DIM
SB_SCALE = SB_HEAD_DIM ** -0.5
Q_BLOCK = 128
POOL_WINDOWS = (2, 4, 8, 16)
POOL_GROUPS = len(POOL_WINDOWS)
POOL_GROUP_DIM = D_MODEL // POOL_GROUPS
D_FF = 2 * D_MODEL
DEEPNORM_ALPHA = (2 * DEPTH) ** 0.25
DEEPNORM_BETA = (8 * DEPTH) ** -0.25
FFN_RESIDUAL = 0.5
LN_EPS = 1e-5

kernel_name = 'hybrid_stickbreak_pool_macaron_deepnorm'


def layer_norm(x, g, b):
    xf = x.astype(jnp.float32)
    mu = jnp.mean(xf, axis=-1, keepdims=True)
    var = jnp.mean(jnp.square(xf - mu), axis=-1, keepdims=True)
    y = (xf - mu) * lax.rsqrt(var + LN_EPS)
    return (y * g.astype(jnp.float32) + b.astype(jnp.float32)).astype(x.dtype)


def swiglu(x, w_in, w_out):
    a, gate = jnp.split(x @ w_in, 2, axis=-1)
    return (jax.nn.silu(a) * gate) @ w_out


def stick_breaking_attention(x, w_qkv, w_o):
    bsz, seq, _ = x.shape
    n_blk = seq // Q_BLOCK
    q, k, v = jnp.split(x @ w_qkv, 3, axis=-1)
    heads = lambda t: t.reshape(bsz, seq, SB_HEADS, SB_HEAD_DIM).transpose(0, 2, 1, 3)
    q, k, v = heads(q), heads(k), heads(v)
    outs = []
    for blk in range(n_blk):
        start, end = blk * Q_BLOCK, (blk + 1) * Q_BLOCK
        q_blk = q[:, :, start:end]
        k_past, v_past = k[:, :, :end], v[:, :, :end]
        z = jnp.einsum('bhqd,bhkd->bhqk', q_blk, k_past).astype(jnp.float32) * SB_SCALE
        past = jnp.arange(end)[None, :] < (start + jnp.arange(Q_BLOCK))[:, None]
        log_keep = jnp.where(past, jax.nn.log_sigmoid(-z), 0.0)
        log_w = z + lax.cumsum(log_keep, axis=3, reverse=True)
        weights = jnp.where(past, jnp.exp(jnp.minimum(log_w, 0.0)), 0.0)
        outs.append(jnp.einsum('bhqk,bhkd->bhqd', weights.astype(v.dtype), v_past))
    o = jnp.concatenate(outs, axis=2)
    o = o.transpose(0, 2, 1, 3).reshape(bsz, seq, SB_INNER)
    return o @ w_o


def causal_window_mean_minus_self(u, window):
    seq = u.shape[1]
    csum = jnp.cumsum(u.astype(jnp.float32), axis=1)
    lagged = jnp.pad(csum, ((0, 0), (window, 0), (0, 0)))[:, :seq]
    count = jnp.minimum(jnp.arange(1, seq + 1), window).astype(jnp.float32)
    return ((csum - lagged) / count[None, :, None] - u.astype(jnp.float32)).astype(u.dtype)


def pool_mixer(x, w_in, w_grp, scale, w_out):
    bsz, seq, _ = x.shape
    u, g = jnp.split(x @ w_in, 2, axis=-1)
    groups = jnp.split(u, POOL_GROUPS, axis=-1)
    pooled = jnp.stack([causal_window_mean_minus_self(ug, w) for ug, w in zip(groups, POOL_WINDOWS)], axis=2)
    mixed = jnp.einsum('bsgc,gcd->bsgd', pooled, w_grp).reshape(bsz, seq, D_MODEL)
    return (scale * mixed * jax.nn.silu(g)) @ w_out


def setup_inputs(seed: int = 0) -> dict:
    key = jax.random.key(seed)
    keys = iter(jax.random.split(key, 64))
    f32 = jnp.float32

    def dense(fan_in, shape, gain=1.0):
        return jax.random.normal(next(keys), shape, f32) * (gain * fan_in ** -0.5)

    def gain_vec(n, noise):
        return 1.0 + noise * jax.random.normal(next(keys), (n,), f32)

    def bias_vec(n):
        return 0.02 * jax.random.normal(next(keys), (n,), f32)

    inputs = {'x': jax.random.normal(next(keys), (BATCH, SEQ, D_MODEL), f32)}
    for i in range(DEPTH):
        p = f'l{i}_'
        inputs[p + 'ffn1_w_in'] = dense(D_MODEL, (D_MODEL, 2 * D_FF))
        inputs[p + 'ffn1_w_out'] = dense(D_FF, (D_FF, D_MODEL), DEEPNORM_BETA)
        inputs[p + 'ln1_g'] = gain_vec(D_MODEL, 0.02)
        inputs[p + 'ln1_b'] = bias_vec(D_MODEL)
        if i % N_MIXERS == 0:
            inputs[p + 'sb_w_qkv'] = dense(D_MODEL, (D_MODEL, 3 * SB_INNER))
            inputs[p + 'sb_w_o'] = dense(SB_INNER, (SB_INNER, D_MODEL), DEEPNORM_BETA)
        else:
            inputs[p + 'pool_w_in'] = dense(D_MODEL, (D_MODEL, 2 * D_MODEL))
            inputs[p + 'pool_w_grp'] = dense(POOL_GROUP_DIM, (POOL_GROUPS, POOL_GROUP_DIM, POOL_GROUP_DIM))
            inputs[p + 'pool_scale'] = gain_vec(D_MODEL, 0.1)
            inputs[p + 'pool_w_out'] = dense(D_MODEL, (D_MODEL, D_MODEL), DEEPNORM_BETA)
        inputs[p + 'ln2_g'] = gain_vec(D_MODEL, 0.02)
        inputs[p + 'ln2_b'] = bias_vec(D_MODEL)
        inputs[p + 'ffn2_w_in'] = dense(D_MODEL, (D_MODEL, 2 * D_FF))
        inputs[p + 'ffn2_w_out'] = dense(D_FF, (D_FF, D_MODEL), DEEPNORM_BETA)
        inputs[p + 'ln3_g'] = gain_vec(D_MODEL, 0.02)
        inputs[p + 'ln3_b'] = bias_vec(D_MODEL)
    return inputs


def reference(x,
              l0_ffn1_w_in, l0_ffn1_w_out, l0_ln1_g, l0_ln1_b,
              l0_sb_w_qkv, l0_sb_w_o, l0_ln2_g, l0_ln2_b,
              l0_ffn2_w_in, l0_ffn2_w_out, l0_ln3_g, l0_ln3_b,
              l1_ffn1_w_in, l1_ffn1_w_out, l1_ln1_g, l1_ln1_b,
              l1_pool_w_in, l1_pool_w_grp, l1_pool_scale, l1_pool_w_out, l1_ln2_g, l1_ln2_b,
              l1_ffn2_w_in, l1_ffn2_w_out, l1_ln3_g, l1_ln3_b):
    ffn1 = [(l0_ffn1_w_in, l0_ffn1_w_out), (l1_ffn1_w_in, l1_ffn1_w_out)]
    ffn2 = [(l0_ffn2_w_in, l0_ffn2_w_out), (l1_ffn2_w_in, l1_ffn2_w_out)]
    ln1 = [(l0_ln1_g, l0_ln1_b), (l1_ln1_g, l1_ln1_b)]
    ln2 = [(l0_ln2_g, l0_ln2_b), (l1_ln2_g, l1_ln2_b)]
    ln3 = [(l0_ln3_g, l0_ln3_b), (l1_ln3_g, l1_ln3_b)]
    mixers = [
        lambda h: stick_breaking_attention(h, l0_sb_w_qkv, l0_sb_w_o),
        lambda h: pool_mixer(h, l1_pool_w_in, l1_pool_w_grp, l1_pool_scale, l1_pool_w_out),
    ]
    for i in range(DEPTH):
        x = layer_norm(DEEPNORM_ALPHA * x + FFN_RESIDUAL * swiglu(x, *ffn1[i]), *ln1[i])
        x = layer_norm(DEEPNORM_ALPHA * x + mixers[i](x), *ln2[i])
        x = layer_norm(DEEPNORM_ALPHA * x + FFN_RESIDUAL * swiglu(x, *ffn2[i]), *ln3[i])
    return x
```

```python
import contextlib
import numpy as np
import ml_dtypes
import concourse.bass as bass
import concourse.mybir as mybir
from concourse.bass_utils import run_bass_kernel_spmd

F32 = mybir.dt.float32
BF16 = mybir.dt.bfloat16
AF = mybir.ActivationFunctionType
ALU = mybir.AluOpType

D = 2048
FF = 4096
T = 512
NT = 8
NCH = 16
HEADS = 8
ALPHA = 4.0 ** 0.25
INV_A = 1.0 / ALPHA
EPS_P = 1e-5 / (ALPHA * ALPHA)
SB_SCALE = 128.0 ** -0.5
POOL_W = (2, 4, 8, 16)

EPOCH = 16000
EPOCH_D = 2000
COMPUTE = ("pe", "act", "dve", "pool")


class Op:
    __slots__ = ("eng", "fn", "reads", "writes", "dsem", "deps", "marked", "ticket", "idx", "is_dma", "bar")

    def __init__(self, eng, fn, reads, writes, dsem):
        self.eng = eng
        self.fn = fn
        self.reads = reads
        self.writes = writes
        self.dsem = dsem
        self.is_dma = dsem is not None
        self.deps = []
        self.marked = False
        self.ticket = None
        self.bar = False


class Sched:
    def __init__(self, nc):
        self.nc = nc
        self.ops = []
        self.barriers = []

    def barrier(self):
        self.barriers.append(len(self.ops))

    def add(self, eng, fn, reads=(), writes=(), dsem=None):
        op = Op(eng, fn, tuple(reads), tuple(writes), dsem)
        self.ops.append(op)
        return op

    def _analyze(self):
        last_w = {}
        readers = {}
        bars = set(self.barriers)
        last_eng = {}
        last_dma = {}
        pending = {}
        for i, op in enumerate(self.ops):
            op.idx = i
            deps = set()
            if i in bars:
                bset = list(last_eng.values()) + list(last_dma.values())
                pending = {e: bset for e in COMPUTE + ("sp",)}
            if op.eng in pending:
                deps.update(pending.pop(op.eng))
            for k in op.reads:
                w = last_w.get(k)
                if w is not None:
                    deps.add(w)
            for k in op.writes:
                w = last_w.get(k)
                if w is not None:
                    deps.add(w)
                for r in readers.get(k, ()):
                    deps.add(r)
            deps.discard(op)
            filt = []
            for d in deps:
                if d.eng == op.eng and not d.is_dma and not op.is_dma:
                    if op.eng == "pe":
                        continue
                    raw = any(k in d.writes for k in op.reads) or any(k in d.writes for k in op.writes)
                    if not raw:
                        continue
                filt.append(d)
            op.deps = filt
            if op.is_dma:
                last_dma[op.dsem] = op
            else:
                last_eng[op.eng] = op
            for k in op.reads:
                readers.setdefault(k, []).append(op)
            for k in op.writes:
                last_w[k] = op
                readers[k] = []

    def emit(self, final_waits=()):
        nc = self.nc
        self._analyze()
        dcount = {}
        for op in self.ops:
            if op.is_dma:
                dcount[op.dsem] = dcount.get(op.dsem, 0) + 1
                op.ticket = dcount[op.dsem]
        for op in self.ops:
            for d in op.deps:
                if not d.is_dma:
                    d.marked = True
        ccount = {e: 0 for e in COMPUTE + ("sp",)}
        for op in self.ops:
            if not op.is_dma and op.marked:
                ccount[op.eng] += 1
                op.ticket = ccount[op.eng]
        stack = contextlib.ExitStack()
        esems = {}
        for e, c in ccount.items():
            n_ep = (c + EPOCH - 1) // EPOCH
            esems[e] = [stack.enter_context(nc.semaphore(f"s_{e}_{i}")) for i in range(n_ep)]
        dsems = {}
        for name, c in dcount.items():
            n_ep = (c + EPOCH_D - 1) // EPOCH_D
            dsems[name] = [stack.enter_context(nc.semaphore(f"d_{name}_{i}")) for i in range(n_ep)]
        self.n_sems = sum(len(v) for v in esems.values()) + sum(len(v) for v in dsems.values())
        run = {}
        dma_before = [None] * len(self.ops)
        for i, op in enumerate(self.ops):
            need = set(d.dsem for d in op.deps if d.is_dma)
            if need:
                dma_before[i] = {s: run.get(s, 0) for s in need}
            if op.is_dma:
                run[op.dsem] = run.get(op.dsem, 0) + 1
        by_eng = {}
        for op in self.ops:
            by_eng.setdefault(op.eng, []).append(op)

        def tick_c(eng, t):
            ep = (t - 1) // EPOCH
            return esems[eng][ep], ep, (t - 1) % EPOCH + 1

        def tick_d(name, t):
            ep = (t - 1) // EPOCH_D
            return dsems[name][ep], ep, ((t - 1) % EPOCH_D + 1) * 16

        def run_engine(ename, e):
            waited = {}
            for op in by_eng.get(ename, []):
                needs = {}
                for d in op.deps:
                    if d.is_dma:
                        t = dma_before[op.idx][d.dsem]
                        key = ("d", d.dsem)
                    else:
                        t = d.ticket
                        key = ("c", d.eng)
                    if t > needs.get(key, 0):
                        needs[key] = t
                for key, t in needs.items():
                    if waited.get(key, 0) >= t:
                        continue
                    waited[key] = t
                    if key[0] == "d":
                        sem, ep, val = tick_d(key[1], t)
                        if ep > 0 and (t - 1) % EPOCH_D < 64:
                            e.wait_ge(dsems[key[1]][ep - 1], EPOCH_D * 16)
                    else:
                        sem, ep, val = tick_c(key[1], t)
                    e.wait_ge(sem, val)
                ins = op.fn(e)
                if op.is_dma:
                    sem, ep, val = tick_d(op.dsem, op.ticket)
                    ins.then_inc(sem, 16)
                elif op.marked:
                    sem, ep, val = tick_c(ename, op.ticket)
                    ins.then_inc(sem, 1)
            for (who, name) in final_waits:
                if who == ename and name in dcount:
                    sem, ep, val = tick_d(name, dcount[name])
                    e.wait_ge(sem, val)

        with nc.Block() as block:
            @block.tensor
            def _(e):
                run_engine("pe", e)

            @block.scalar
            def _(e):
                run_engine("act", e)

            @block.vector
            def _(e):
                run_engine("dve", e)

            @block.gpsimd
            def _(e):
                run_engine("pool", e)

            @block.sync
            def _(e):
                run_engine("sp", e)
        stack.close()


def _call(meth, *a, **k):
    return lambda e: getattr(e, meth)(*a, **k)


VEC_NAMES = ["l0_ln1_g", "l0_ln1_b", "l0_ln2_g", "l0_ln2_b", "l0_ln3_g", "l0_ln3_b",
             "l1_ln1_g", "l1_ln1_b", "l1_ln2_g", "l1_ln2_b", "l1_ln3_g", "l1_ln3_b", "l1_pool_scale"]
VIDX = {n: i for i, n in enumerate(VEC_NAMES)}


class Ctx:
    def __init__(self, nc, st, nt):
        self.nc = nc
        self.st = st
        self.S = Sched(nc)
        self.nt = nt
        self.wb = {}
        self.nslab = 0
        self.dram_in = {}
        self.store_eng = "sp"
        self.deferred = []

    def sb(self, name, shape, dt):
        return self.st.enter_context(self.nc.sbuf_tensor(name, shape, dt))

    def din(self, name, shape, dt):
        ap = self.nc.dram_tensor(name, list(shape), dt, kind="ExternalInput").ap()
        self.dram_in[name] = ap
        return ap

    def dout(self, name, shape, dt):
        return self.nc.dram_tensor(name, list(shape), dt, kind="ExternalOutput").ap()

    def dint(self, name, shape, dt):
        return self.nc.dram_tensor(name, list(shape), dt, kind="Internal").ap()

    def alloc_common(self):
        nc = self.nc
        self.X = self.sb("X_sb", [128, NCH, T], F32)
        self.Xb = self.sb("Xb", [128, NCH, T], BF16)
        self.Hraw = self.sb("H", [128, 32 * T], BF16)
        self.H = self.Hraw[:].rearrange("p (c t) -> p c t", t=T)
        self.slabs = [self.sb(f"slab{i}", [128, 4096], BF16) for i in range(4)]
        self.SA = [self.sb(f"SA{i}", [128, T], F32) for i in range(2)]
        self.SQ = [self.sb(f"SQ{i}", [128, T], F32) for i in range(2)]
        self.mean = self.sb("mean", [128, T], F32)
        self.rstd = self.sb("rstd", [128, T], F32)
        self.tmpn = [self.sb(f"tmpn{i}", [128, T], F32) for i in range(3)]
        self.vecs = self.sb("vecs_sb", [128, len(VEC_NAMES) * NCH], F32)
        self.ones_f = self.sb("ones_f", [128, 128], F32)
        self.ident = self.sb("ident", [128, 128], F32)
        self.ps2 = [self.st.enter_context(nc.psum_tensor(f"ps{i}", [128, 2 * T], F32)) for i in range(4)]
        self.ps = [self.ps2[i // 2][:, (i % 2) * T:(i % 2 + 1) * T] for i in range(8)]
        S = self.S
        S.add("pool", _call("memset", self.ones_f[:], 1.0), writes=["ones_f"])
        S.add("pool", _call("memset", self.ident[:], 0.0), writes=["ident"])
        S.add("pool", _call("affine_select", out=self.ident[:], in_=self.ident[:], pattern=[[-1, 128]],
                            compare_op=ALU.not_equal, fill=1.0, base=0, channel_multiplier=1),
              reads=["ident"], writes=["ident"])
        vin = self.din("vecs", [128, len(VEC_NAMES) * NCH], F32)
        S.add("sp", _call("dma_start", out=self.vecs[:], in_=vin), writes=["vecs"], dsem="vecs")

    def vec(self, name, c):
        i = VIDX[name] * NCH + c
        return self.vecs[:, i:i + 1]

    def emit_deferred_cast(self):
        if self.deferred:
            name, dst, src, r0, rows = self.deferred.pop(0)
            self.S.add("pool", _call("dma_start", out=dst[r0:r0 + rows, :], in_=src[r0:r0 + rows, :], max_dma_last_dim=8192),
                       writes=[("wb", name)], dsem="cast_" + name)

    def cast_weight(self, name, shape, defer=False):
        K, N = shape
        src = self.din(name, [K, N], F32)
        dst = self.dint(name + "_bf", [K, N], BF16)
        self.wb[name] = dst
        rows = 128
        while rows * 2 * N <= (1 << 20) and K % (rows * 2) == 0:
            rows *= 2
        import os
        if os.environ.get("NOCAST"):
            return
        for r0 in range(0, K, rows):
            if defer:
                self.deferred.append((name, dst, src, r0, rows))
                continue
            self.S.add("pool", _call("dma_start", out=dst[r0:r0 + rows, :], in_=src[r0:r0 + rows, :], max_dma_last_dim=8192),
                       writes=[("wb", name)], dsem="cast_" + name)

    def next_slab(self):
        i = self.nslab % len(self.slabs)
        self.nslab += 1
        return i, self.slabs[i]

    def ffn(self, w_in, w_out):
        S = self.S
        win_v = self.wb[w_in].rearrange("(kc p) n -> p kc n", p=128)
        wout_v = self.wb[w_out].rearrange("(kc p) n -> p kc n", p=128)
        for ffc in range(32):
            si, slab = self.next_slab()
            sv = slab[:].rearrange("p (k n) -> p k n", n=256)
            S.add("sp", _call("dma_start", out=sv[:, :, 0:128], in_=win_v[:, :, ffc * 128:(ffc + 1) * 128]),
                  reads=[("wb", w_in)], writes=[("slab", si)], dsem=f"slab{si}")
            S.add("sp", _call("dma_start", out=sv[:, :, 128:256], in_=win_v[:, :, FF + ffc * 128:FF + (ffc + 1) * 128]),
                  reads=[("wb", w_in)], writes=[("slab", si)], dsem=f"slab{si}")
            pa, pg = 2 * (ffc % 2), 2 * (ffc % 2) + 1
            for half, pb in ((0, pa), (1, pg)):
                for kc in range(NCH):
                    S.add("pe", _call("matmul", self.ps[pb][:], lhsT=sv[:, kc, half * 128:(half + 1) * 128],
                                      rhs=self.Xb[:, kc, :], start=(kc == 0), stop=(kc == NCH - 1)),
                          reads=[("slab", si), ("Xb", kc)], writes=[("ps", pb)])
            sa = self.SA[ffc % 2]
            S.add("act", _call("activation", out=sa[:], in_=self.ps[pa][:], func=AF.Silu),
                  reads=[("ps", pa)], writes=[("SA", ffc % 2)])
            S.add("dve", _call("tensor_tensor", out=self.H[:, ffc, :], in0=sa[:], in1=self.ps[pg][:], op=ALU.mult),
                  reads=[("SA", ffc % 2), ("ps", pg)], writes=[("H", ffc)])
        for oc in range(NCH):
            si, slab = self.next_slab()
            sv = slab[:].rearrange("p (k n) -> p k n", n=128)
            S.add("sp", _call("dma_start", out=sv, in_=wout_v[:, :, oc * 128:(oc + 1) * 128]),
                  reads=[("wb", w_out)], writes=[("slab", si)], dsem=f"slab{si}")
            pb = 4 + (oc % 2)
            for kc in range(32):
                S.add("pe", _call("matmul", self.ps[pb][:], lhsT=sv[:, kc, :], rhs=self.H[:, kc, :],
                                  start=(kc == 0), stop=(kc == 31)),
                      reads=[("slab", si), ("H", kc)], writes=[("ps", pb)])
            S.add("dve", _call("scalar_tensor_tensor", out=self.X[:, oc, :], in0=self.ps[pb][:], scalar=0.5 * INV_A,
                               in1=self.X[:, oc, :], op0=ALU.mult, op1=ALU.add),
                  reads=[("ps", pb), ("X", oc)], writes=[("X", oc)])
            if oc > 0:
                self.ln_stats_chunk(oc - 1)
        self.ln_stats_chunk(NCH - 1)

    def ln_stats_chunk(self, oc):
        S = self.S
        p1, p2 = self.ps[6], self.ps[7]
        sq = self.SQ[oc % 2]
        S.add("act", _call("activation", out=sq[:], in_=self.X[:, oc, :], func=AF.Square),
              reads=[("X", oc)], writes=[("SQ", oc % 2)])
        S.add("pe", _call("matmul", p1[:], lhsT=self.ones_f[:], rhs=self.X[:, oc, :], start=(oc == 0), stop=(oc == NCH - 1)),
              reads=["ones_f", ("X", oc)], writes=[("ps", 6)])
        S.add("pe", _call("matmul", p2[:], lhsT=self.ones_f[:], rhs=sq[:], start=(oc == 0), stop=(oc == NCH - 1)),
              reads=["ones_f", ("SQ", oc % 2)], writes=[("ps", 7)])

    def layernorm(self, gname, bname):
        S = self.S
        p1, p2 = self.ps[6], self.ps[7]
        S.add("dve", _call("tensor_scalar", out=self.mean[:], in0=p1[:], scalar1=1.0 / D, scalar2=None, op0=ALU.mult),
              reads=[("ps", 6)], writes=["mean"])
        t0 = self.rstd
        S.add("dve", _call("tensor_tensor", out=t0[:], in0=self.mean[:], in1=self.mean[:], op=ALU.mult),
              reads=["mean"], writes=["rstd"])
        S.add("dve", _call("scalar_tensor_tensor", out=t0[:], in0=p2[:], scalar=1.0 / D, in1=t0[:], op0=ALU.mult, op1=ALU.subtract),
              reads=[("ps", 7), "rstd"], writes=["rstd"])
        S.add("dve", _call("tensor_scalar", out=t0[:], in0=t0[:], scalar1=EPS_P, scalar2=None, op0=ALU.add),
              reads=["rstd"], writes=["rstd"])
        S.add("act", _call("activation", out=t0[:], in_=t0[:], func=AF.Sqrt),
              reads=["rstd"], writes=["rstd"])
        S.add("dve", _call("reciprocal", out=p1[:], in_=t0[:]), reads=["rstd"], writes=[("ps", 6)])
        S.add("dve", _call("scalar_tensor_tensor", out=p2[:], in0=self.mean[:], scalar=-1.0, in1=p1[:],
                           op0=ALU.mult, op1=ALU.mult), reads=["mean", ("ps", 6)], writes=[("ps", 7)])
        for oc in range(NCH):
            tn = self.tmpn[oc % 3]
            S.add("dve", _call("tensor_tensor", out=tn[:], in0=self.X[:, oc, :], in1=p1[:], op=ALU.mult),
                  reads=[("X", oc), ("ps", 6)], writes=[("tmpn", oc % 3)])
            S.add("dve", _call("tensor_tensor", out=tn[:], in0=tn[:], in1=p2[:], op=ALU.add),
                  reads=[("tmpn", oc % 3), ("ps", 7)], writes=[("tmpn", oc % 3)])
            S.add("act", _call("activation", out=self.X[:, oc, :], in_=tn[:], func=AF.Identity,
                               scale=self.vec(gname, oc), bias=self.vec(bname, oc)),
                  reads=[("tmpn", oc % 3), "vecs"], writes=[("X", oc)])
            if oc % 2 == 0:
                S.add("pool", _call("tensor_copy", out=self.Xb[:, oc, :], in_=self.X[:, oc, :]),
                      reads=[("X", oc)], writes=[("Xb", oc)])
            else:
                S.add("act", _call("activation", out=self.Xb[:, oc, :], in_=tn[:], func=AF.Identity,
                                   scale=self.vec(gname, oc), bias=self.vec(bname, oc)),
                      reads=[("tmpn", oc % 3), "vecs"], writes=[("Xb", oc)])

    def load_X(self, src, j, key):
        v = src.rearrange("(c p) t -> p c t", p=128)
        self.S.add("sp", _call("dma_start", out=self.X[:], in_=v[:, :, j * T:(j + 1) * T]),
                   reads=[key], writes=[("X", oc) for oc in range(NCH)], dsem="Xld")

    def store_X(self, dst, j, key, dsem="Xst"):
        v = dst.rearrange("(c p) t -> p c t", p=128)
        self.S.add(self.store_eng, _call("dma_start", out=v[:, :, j * T:(j + 1) * T], in_=self.X[:]),
                   reads=[("X", oc) for oc in range(NCH)], writes=[key], dsem=dsem)

    def cast_Xb(self):
        for oc in range(NCH):
            self.S.add("pool" if oc % 2 == 0 else "dve", _call("tensor_copy", out=self.Xb[:, oc, :], in_=self.X[:, oc, :]),
                       reads=[("X", oc)], writes=[("Xb", oc)])


def build_l1(nt=NT, stage=99):
    nc = bass.Bass("TRN2", target_bir_lowering=False)
    st = contextlib.ExitStack()
    with st:
        C = Ctx(nc, st, nt)
        S = C.S
        C.alloc_common()
        x_tm = C.din("x_own", [nt * T, D], F32)
        x1T = C.dout("x1T", [D, nt * T], F32)
        qT = C.dout("qT", [HEADS, 128, nt * T], BF16)
        kT = C.dout("kT", [HEADS, 128, nt * T], BF16)
        vO = C.dout("v", [HEADS, 128, nt * 4, 128], BF16)
        C.cast_weight("l0_ffn1_w_in", (D, 2 * FF))
        C.cast_weight("l0_ffn1_w_out", (FF, D))
        C.cast_weight("l0_sb_w_qkv", (D, 3072))
        TM = [C.sb(f"TM{i}", [128, D], F32) for i in range(2)]
        qsb = C.sb("qsb", [128, HEADS, T], BF16)
        ksb = C.sb("ksb", [128, HEADS, T], BF16)
        vsb = C.sb("vsb", [128, 4, 1024], BF16)
        wqkv_v = None
        for j in range(nt):
            for tb in range(4):
                tm = TM[tb % 2]
                r0 = j * T + tb * 128
                S.add("sp", _call("dma_start", out=tm[:], in_=x_tm[r0:r0 + 128, :]), writes=[("TM", tb % 2)], dsem=f"TM{tb % 2}")
                import os
                BIS = int(os.environ.get("BIS", "9"))
                for og in range(4):
                    if BIS < 1:
                        break
                    pb = og % 2
                    for q in range(4):
                        oc = og * 4 + q
                        S.add("pe", _call("transpose", C.ps[pb][:, q * 128:(q + 1) * 128], tm[:, oc * 128:(oc + 1) * 128], C.ident[:]),
                              reads=[("TM", tb % 2), "ident"], writes=[("ps", pb)])
                    pv = C.ps[pb][:].rearrange("p (q t) -> p q t", t=128)
                    if BIS < 2:
                        continue
                    S.add("dve", _call("tensor_copy", out=C.X[:, og * 4:(og + 1) * 4, tb * 128:(tb + 1) * 128], in_=pv),
                          reads=[("ps", pb)], writes=[("X", og * 4 + q) for q in range(4)])
                    if BIS < 3:
                        continue
                    S.add("act", _call("activation", out=C.Xb[:, og * 4:(og + 1) * 4, tb * 128:(tb + 1) * 128],
                                       in_=C.X[:, og * 4:(og + 1) * 4, tb * 128:(tb + 1) * 128], func=AF.Copy),
                          reads=[("X", og * 4 + q) for q in range(4)], writes=[("Xb", og * 4 + q) for q in range(4)])
            if stage >= 2:
                C.ffn("l0_ffn1_w_in", "l0_ffn1_w_out")
            if stage >= 3:
                C.layernorm("l0_ln1_g", "l0_ln1_b")
            C.store_X(x1T, j, ("x1T", j), dsem="out")
            if stage < 4:
                continue
            wq_v = C.wb["l0_sb_w_qkv"].rearrange("(kc p) n -> p kc n", p=128)
            for oc in range(16):
                si, slab = C.next_slab()
                sv = slab[:, 0:2048].rearrange("p (k n) -> p k n", n=128)
                S.add("sp", _call("dma_start", out=sv, in_=wq_v[:, :, oc * 128:(oc + 1) * 128]),
                      reads=[("wb", "l0_sb_w_qkv")], writes=[("slab", si)], dsem=f"slab{si}")
                pb = oc % 2
                for kc in range(NCH):
                    S.add("pe", _call("matmul", C.ps[pb][:], lhsT=sv[:, kc, :], rhs=C.Xb[:, kc, :], start=(kc == 0), stop=(kc == NCH - 1)),
                          reads=[("slab", si), ("Xb", kc)], writes=[("ps", pb)])
                if oc < 8:
                    S.add("act", _call("activation", out=qsb[:, oc, :], in_=C.ps[pb][:], func=AF.Copy, scale=SB_SCALE),
                          reads=[("ps", pb)], writes=["qsb"])
                else:
                    S.add("dve", _call("tensor_copy", out=ksb[:, oc - 8, :], in_=C.ps[pb][:]),
                          reads=[("ps", pb)], writes=["ksb"])
            S.add("sp", _call("dma_start", out=qT.rearrange("h d t -> d h t")[:, :, j * T:(j + 1) * T], in_=qsb[:]),
                  reads=["qsb"], writes=[("qT", j)], dsem="out")
            S.add("sp", _call("dma_start", out=kT.rearrange("h d t -> d h t")[:, :, j * T:(j + 1) * T], in_=ksb[:]),
                  reads=["ksb"], writes=[("kT", j)], dsem="out")
            wv = C.Hraw[:].rearrange("p (k n) -> p k n", n=1024)
            S.add("sp", _call("dma_start", out=wv, in_=wq_v[:, :, 2048:3072]),
                  reads=[("wb", "l0_sb_w_qkv")] + [("H", c) for c in range(32)], writes=[("H", c) for c in range(32)], dsem="wv")
            for tb in range(4):
                for half in range(2):
                    pb = 2 + (tb * 2 + half) % 2
                    for kc in range(NCH):
                        S.add("pe", _call("matmul", C.ps[pb][:], lhsT=C.Xb[:, kc, tb * 128:(tb + 1) * 128],
                                          rhs=wv[:, kc, half * 512:(half + 1) * 512], start=(kc == 0), stop=(kc == NCH - 1)),
                              reads=[("H", c) for c in range(32)] + [("Xb", kc)], writes=[("ps", pb)])
                    eng = "act" if half == 0 else "dve"
                    if eng == "act":
                        S.add("act", _call("activation", out=vsb[:, tb, half * 512:(half + 1) * 512], in_=C.ps[pb][:], func=AF.Copy),
                              reads=[("ps", pb)], writes=["vsb"])
                    else:
                        S.add("dve", _call("tensor_copy", out=vsb[:, tb, half * 512:(half + 1) * 512], in_=C.ps[pb][:]),
                              reads=[("ps", pb)], writes=["vsb"])
            for tb in range(4):
                vdst = vO.rearrange("h p b d -> p b h d")[:, j * 4 + tb, :, :]
                S.add("sp", _call("dma_start", out=vdst, in_=vsb[:, tb, :].rearrange("p (h d) -> p h d", d=128)),
                      reads=["vsb"], writes=[("v", j, tb)], dsem="out")
        S.emit(final_waits=[("sp", "out")])
    return nc


def pack_vecs(inputs):
    cols = []
    for n in VEC_NAMES:
        v = np.asarray(inputs[n], dtype=np.float32).reshape(NCH, 128).T
        cols.append(v)
    return np.ascontiguousarray(np.concatenate(cols, axis=1))


def emit_attention(C, j, qT, kT_all, v_all, A):
    S = C.S
    qsb = A["qsb"]
    S.add("sp", _call("dma_start", out=qsb[:], in_=qT.rearrange("h d t -> d h t")[:, :, j * T:(j + 1) * T]),
          reads=[("qT", j)], writes=["qsb"], dsem="qsb")
    for h in range(HEADS):
        po = 4 + (h % 2)
        blocks = []
        for jj in range(j, -1, -1):
            for cp in range(3, -1, -1):
                for kb in range(3, -1, -1):
                    blocks.append((jj, cp, kb))
        n = len(blocks)
        grp_of = {}

        def load_group(jj):
            gi = A["gcount"] % 3
            A["gcount"] += 1
            kg, kkeys = A["Kg"][gi]
            vg, vkeys = A["Vg"][gi]
            S.add("sp", _call("dma_start", out=kg, in_=kT_all[:, h, :, jj * T:(jj + 1) * T].rearrange("r d t -> d r t")),
                  reads=["kT_all"], writes=kkeys, dsem=f"Kg{gi}")
            S.add("sp", _call("dma_start", out=vg, in_=v_all[:, h, :, jj * 4:(jj + 1) * 4, :].rearrange("r p b d -> p r b d")),
                  reads=["v_all"], writes=vkeys, dsem=f"Vg{gi}")
            grp_of[jj] = (kg, kkeys, vg, vkeys)

        def stage_S(i):
            jj, cp, kb = blocks[i]
            if jj not in grp_of:
                load_group(jj)
            kg, kkeys, vg, vkeys = grp_of[jj]
            pb = i % 2
            S.add("pe", _call("matmul", C.ps[pb][:], lhsT=kg[:, cp, kb * 128:(kb + 1) * 128], rhs=qsb[:, h, :], start=True, stop=True),
                  reads=kkeys + ["qsb"], writes=[("ps", pb)])

        def stage_B(i):
            jj, cp, kb = blocks[i]
            pb = i % 2
            e = A["E"][i % 2]
            lp = A["Lp"][i % 3]
            S.add("act", _call("activation", out=e[:], in_=C.ps[pb][:], func=AF.Exp), reads=[("ps", pb)], writes=[("E", i % 2)])
            S.add("act", _call("activation", out=lp[:], in_=e[:], func=AF.Ln, bias=A["one"][:, 0:1]),
                  reads=[("E", i % 2), "one"], writes=[("Lp", i % 3)])
            if jj == j:
                S.add("dve", _call("tensor_tensor", out=lp[:], in0=lp[:], in1=A["msk"][:, cp * 4 + kb, :], op=ALU.mult),
                      reads=[("Lp", i % 3), "msk"], writes=[("Lp", i % 3)])

        def stage_C(i):
            jj, cp, kb = blocks[i]
            kg, kkeys, vg, vkeys = grp_of[jj]
            pw = 2 + (i % 2)
            lp = A["Lp"][i % 3]
            S.add("pe", _call("matmul", C.ps[pw][:], lhsT=kg[:, cp, kb * 128:(kb + 1) * 128], rhs=qsb[:, h, :], start=True, stop=False),
                  reads=kkeys + ["qsb"], writes=[("ps", pw)])
            S.add("pe", _call("matmul", C.ps[pw][:], lhsT=A["negtri"][:], rhs=lp[:], start=False, stop=(i == 0)),
                  reads=["negtri", ("Lp", i % 3)], writes=[("ps", pw)])
            if i > 0:
                S.add("pe", _call("matmul", C.ps[pw][:], lhsT=A["negones"][:], rhs=A["Lsb"][(i - 1) % 3][:], start=False, stop=True),
                      reads=["negones", ("Lsb", (i - 1) % 3)], writes=[("ps", pw)])

        def stage_D(i):
            jj, cp, kb = blocks[i]
            pw = 2 + (i % 2)
            w = A["W"][i % 3]
            S.add("act", _call("activation", out=w[:], in_=C.ps[pw][:], func=AF.Exp), reads=[("ps", pw)], writes=[("W", i % 3)])
            if jj == j:
                S.add("dve", _call("scalar_tensor_tensor", out=w[:], in0=w[:], scalar=1.0, in1=A["msk"][:, cp * 4 + kb, :],
                                   op0=ALU.min, op1=ALU.mult), reads=[("W", i % 3), "msk"], writes=[("W", i % 3)])
            else:
                S.add("dve", _call("tensor_scalar", out=w[:], in0=w[:], scalar1=1.0, scalar2=None, op0=ALU.min),
                      reads=[("W", i % 3)], writes=[("W", i % 3)])

        def stage_E(i):
            if i == n - 1:
                return
            lp = A["Lp"][i % 3]
            if i == 0:
                S.add("pool", _call("tensor_copy", out=A["Lsf"][:], in_=lp[:]), reads=[("Lp", i % 3)], writes=["Lsf"])
            else:
                S.add("pool", _call("tensor_tensor", out=A["Lsf"][:], in0=A["Lsf"][:], in1=lp[:], op=ALU.add),
                      reads=["Lsf", ("Lp", i % 3)], writes=["Lsf"])
            S.add("pool", _call("tensor_copy", out=A["Lsb"][i % 3][:], in_=A["Lsf"][:]), reads=["Lsf"], writes=[("Lsb", i % 3)])

        def stage_F(i):
            jj, cp, kb = blocks[i]
            kg, kkeys, vg, vkeys = grp_of[jj]
            w = A["W"][i % 3]
            S.add("pe", _call("matmul", C.ps[po][:], lhsT=vg[:, cp, kb, :], rhs=w[:], start=(i == 0), stop=(i == n - 1)),
                  reads=vkeys + [("W", i % 3)], writes=[("ps", po)])

        for s in range(n + 2):
            if s < n:
                stage_S(s)
                stage_B(s)
            if 0 <= s - 1 < n:
                stage_C(s - 1)
                stage_D(s - 1)
                stage_E(s - 1)
            if 0 <= s - 2 < n:
                stage_F(s - 2)
        osb, okeys = A["osb"]
        S.add("dve", _call("tensor_copy", out=osb[:, h, :], in_=C.ps[po][:]), reads=[("ps", po)], writes=okeys)


def alloc_attention(C):
    S = C.S
    A = {"gcount": 0}
    A["qsb"] = C.sb("qsb", [128, HEADS, T], BF16)
    A["msk"] = C.sb("msk", [128, 16, T], BF16)
    A["E"] = [C.sb(f"E{i}", [128, T], F32) for i in range(2)]
    A["Lp"] = [C.sb(f"Lp{i}", [128, T], BF16) for i in range(3)]
    A["W"] = [C.sb(f"W{i}", [128, T], BF16) for i in range(3)]
    A["Lsf"] = C.sb("Lsf", [128, T], F32)
    A["Lsb"] = [C.sb(f"Lsb{i}", [128, T], BF16) for i in range(3)]
    A["negtri"] = C.sb("negtri", [128, 128], BF16)
    A["negones"] = C.sb("negones", [128, 128], BF16)
    A["one"] = C.sb("one", [128, 1], F32)
    hr = C.Hraw
    A["Kg"] = []
    A["Vg"] = []
    for i in range(3):
        kv = hr[:, i * 2048:(i + 1) * 2048].rearrange("p (r t) -> p r t", t=T)
        A["Kg"].append((kv, [("H", 4 * i + k) for k in range(4)]))
        vv = hr[:, 6144 + i * 2048:6144 + (i + 1) * 2048].rearrange("p (r b d) -> p r b d", b=4, d=128)
        A["Vg"].append((vv, [("H", 12 + 4 * i + k) for k in range(4)]))
    ov = hr[:, 12288:16384].rearrange("p (h t) -> p h t", t=T)
    A["osb"] = (ov, [("H", 24 + k) for k in range(8)])
    S.add("pool", _call("memset", A["negones"][:], -1.0), writes=["negones"])
    S.add("pool", _call("memset", A["one"][:], 1.0), writes=["one"])
    S.add("pool", _call("memset", A["negtri"][:], -1.0), writes=["negtri"])
    S.add("pool", _call("affine_select", out=A["negtri"][:], in_=A["negtri"][:], pattern=[[-1, 128]],
                        compare_op=ALU.is_ge, fill=0.0, base=0, channel_multiplier=1),
          reads=["negtri"], writes=["negtri"])
    return A


def emit_proj_residual(C, wname, nk, src, src_keys):
    S = C.S
    wv = C.wb[wname].rearrange("(kc p) n -> p kc n", p=128)
    for oc in range(NCH):
        si, slab = C.next_slab()
        sv = slab[:, 0:nk * 128].rearrange("p (k n) -> p k n", n=128)
        S.add("sp", _call("dma_start", out=sv, in_=wv[:, :, oc * 128:(oc + 1) * 128]),
              reads=[("wb", wname)], writes=[("slab", si)], dsem=f"slab{si}")
        pb = oc % 2
        for kc in range(nk):
            S.add("pe", _call("matmul", C.ps[pb][:], lhsT=sv[:, kc, :], rhs=src[:, kc, :], start=(kc == 0), stop=(kc == nk - 1)),
                  reads=[("slab", si)] + src_keys(kc), writes=[("ps", pb)])
        S.add("dve", _call("scalar_tensor_tensor", out=C.X[:, oc, :], in0=C.ps[pb][:], scalar=INV_A,
                           in1=C.X[:, oc, :], op0=ALU.mult, op1=ALU.add),
              reads=[("ps", pb), ("X", oc)], writes=[("X", oc)])
        if oc > 0:
            C.ln_stats_chunk(oc - 1)
    C.ln_stats_chunk(NCH - 1)


def build_l2(nt=NT, stage=99):
    nc = bass.Bass("TRN2", target_bir_lowering=False)
    st = contextlib.ExitStack()
    with st:
        C = Ctx(nc, st, nt)
        S = C.S
        C.alloc_common()
        A = alloc_attention(C)
        x1T = C.din("x1T", [D, nt * T], F32)
        qT = C.din("qT", [HEADS, 128, nt * T], BF16)
        kT_all = C.din("kT_all", [4, HEADS, 128, nt * T], BF16)
        v_all = C.din("v_all", [4, HEADS, 128, nt * 4, 128], BF16)
        msk_in = C.din("masks", [128, 16, T], BF16)
        x4T = C.dout("x4T", [D, nt * T], F32)
        uT = C.dout("uT", [D, nt * T], F32)
        sgT = C.dout("sgT", [D, nt * T], F32)
        S.add("sp", _call("dma_start", out=A["msk"][:], in_=msk_in), writes=["msk"], dsem="msk")
        C.cast_weight("l0_sb_w_o", (1024, D))
        if stage >= 2:
            C.cast_weight("l0_ffn2_w_in", (D, 2 * FF))
            C.cast_weight("l0_ffn2_w_out", (FF, D))
        if stage >= 3:
            C.cast_weight("l1_ffn1_w_in", (D, 2 * FF))
            C.cast_weight("l1_ffn1_w_out", (FF, D))
            C.cast_weight("l1_pool_w_in", (D, 2 * D))
        for j in range(nt):
            emit_attention(C, j, qT, kT_all, v_all, A)
            C.load_X(x1T, j, ("x1T", j))
            osb, okeys = A["osb"]
            emit_proj_residual(C, "l0_sb_w_o", 8, osb, lambda kc: [okeys[kc]])
            C.layernorm("l0_ln2_g", "l0_ln2_b")
            if stage >= 2:
                C.ffn("l0_ffn2_w_in", "l0_ffn2_w_out")
                C.layernorm("l0_ln3_g", "l0_ln3_b")
            if stage >= 3:
                C.ffn("l1_ffn1_w_in", "l1_ffn1_w_out")
                C.layernorm("l1_ln1_g", "l1_ln1_b")
            C.store_X(x4T, j, ("x4T", j), dsem="out")
            if stage < 3:
                continue
            wv = C.wb["l1_pool_w_in"].rearrange("(kc p) n -> p kc n", p=128)
            for oc in range(NCH):
                si, slab = C.next_slab()
                sv = slab[:].rearrange("p (k n) -> p k n", n=256)
                S.add("sp", _call("dma_start", out=sv[:, :, 0:128], in_=wv[:, :, oc * 128:(oc + 1) * 128]),
                      reads=[("wb", "l1_pool_w_in")], writes=[("slab", si)], dsem=f"slab{si}")
                S.add("sp", _call("dma_start", out=sv[:, :, 128:256], in_=wv[:, :, D + oc * 128:D + (oc + 1) * 128]),
                      reads=[("wb", "l1_pool_w_in")], writes=[("slab", si)], dsem=f"slab{si}")
                pu, pg = 2 * (oc % 2), 2 * (oc % 2) + 1
                for half, pb in ((0, pu), (1, pg)):
                    for kc in range(NCH):
                        S.add("pe", _call("matmul", C.ps[pb][:], lhsT=sv[:, kc, half * 128:(half + 1) * 128],
                                          rhs=C.Xb[:, kc, :], start=(kc == 0), stop=(kc == NCH - 1)),
                              reads=[("slab", si), ("Xb", kc)], writes=[("ps", pb)])
                su = C.SQ[oc % 2]
                sg = C.SA[oc % 2]
                S.add("dve", _call("tensor_copy", out=su[:], in_=C.ps[pu][:]), reads=[("ps", pu)], writes=[("SQ", oc % 2)])
                S.add("act", _call("activation", out=sg[:], in_=C.ps[pg][:], func=AF.Silu), reads=[("ps", pg)], writes=[("SA", oc % 2)])
                S.add("sp", _call("dma_start", out=uT[oc * 128:(oc + 1) * 128, j * T:(j + 1) * T], in_=su[:]),
                      reads=[("SQ", oc % 2)], writes=[("uT", j, oc)], dsem="out")
                S.add("sp", _call("dma_start", out=sgT[oc * 128:(oc + 1) * 128, j * T:(j + 1) * T], in_=sg[:]),
                      reads=[("SA", oc % 2)], writes=[("sgT", j, oc)], dsem="out")
        S.emit(final_waits=[("sp", "out")])
    return nc


def make_masks(c):
    s = np.arange(128)[:, None, None]
    blk = np.arange(16)[None, :, None]
    t = np.arange(T)[None, None, :]
    kpos = blk * 128 + s
    qpos = c * T + t
    return np.ascontiguousarray((kpos < qpos).astype(np.float32).astype(ml_dtypes.bfloat16))


def alloc_pool(C):
    P = {}
    P["UE"] = [C.sb(f"UE{i}", [128, 4, T + 16], F32) for i in range(2)]
    P["B"] = [C.sb(f"PB{i}", [128, 4, T + 16], F32) for i in range(2)]
    P["fix"] = C.sb("fixsb", [128, C.nt * 4 * 16], F32)
    P["TMo"] = [C.sb(f"TMo{i}", [128, D], F32) for i in range(2)]
    return P


def emit_pool_mixer(C, P, j, uT, sgT, halo):
    S = C.S
    uv = uT.rearrange("(c p) t -> p c t", p=128)
    hv = halo.rearrange("j (c p) t -> j p c t", p=128)
    fixv = P["fix"][:].rearrange("p (j g t) -> p j g t", g=4, t=16)
    wgv = C.wb["l1_pool_w_grp"].rearrange("(kc p) n -> p kc n", p=128)
    for g in range(4):
        w = POOL_W[g]
        ue = P["UE"][g % 2]
        uk = ("UE", g % 2)
        S.add("sp", _call("dma_start", out=ue[:, :, 16:16 + T], in_=uv[:, 4 * g:4 * g + 4, j * T:(j + 1) * T]),
              reads=[("uT", j)], writes=[uk], dsem=f"UE{g % 2}")
        S.add("sp", _call("dma_start", out=ue[:, :, 0:16], in_=hv[j, :, 4 * g:4 * g + 4, :]),
              reads=["halo"], writes=[uk], dsem=f"UE{g % 2}")
        src, sk = ue, uk
        sh = 1
        for lvl in range(g + 1):
            dst, dk = P["B"][lvl % 2], ("PB", lvl % 2)
            lo = 2 * sh - 1
            S.add("pool", _call("tensor_tensor", out=dst[:, :, lo:T + 16], in0=src[:, :, lo:T + 16], in1=src[:, :, lo - sh:T + 16 - sh], op=ALU.add),
                  reads=[sk], writes=[dk])
            src, sk = dst, dk
            sh *= 2
        for c in range(4):
            S.add("pool", _call("tensor_tensor", out=src[:, c, 16:32], in0=src[:, c, 16:32], in1=fixv[:, j, g, :], op=ALU.mult),
                  reads=[sk, "fix"], writes=[sk])
        S.add("dve", _call("scalar_tensor_tensor", out=C.Xb[:, 4 * g:4 * g + 4, :], in0=src[:, :, 16:16 + T], scalar=1.0 / w,
                           in1=ue[:, :, 16:16 + T], op0=ALU.mult, op1=ALU.subtract),
              reads=[sk, uk], writes=[("Xb", 4 * g + c) for c in range(4)])
        si, slab = C.next_slab()
        sv = slab[:, 0:2048].rearrange("p (k n) -> p k n", n=512)
        S.add("sp", _call("dma_start", out=sv, in_=wgv[:, 4 * g:4 * g + 4, :]),
              reads=[("wb", "l1_pool_w_grp")], writes=[("slab", si)], dsem=f"slab{si}")
        for oc in range(4):
            cc = 4 * g + oc
            pb = cc % 2
            for kc in range(4):
                S.add("pe", _call("matmul", C.ps[pb][:], lhsT=sv[:, kc, oc * 128:(oc + 1) * 128], rhs=C.Xb[:, 4 * g + kc, :],
                                  start=(kc == 0), stop=(kc == 3)),
                      reads=[("slab", si), ("Xb", 4 * g + kc)], writes=[("ps", pb)])
            sg = C.SA[cc % 2]
            S.add("sp", _call("dma_start", out=sg[:], in_=sgT[cc * 128:(cc + 1) * 128, j * T:(j + 1) * T]),
                  reads=[("sgT", j)], writes=[("SA", cc % 2)], dsem=f"SA{cc % 2}")
            S.add("dve", _call("scalar_tensor_tensor", out=C.H[:, cc, :], in0=C.ps[pb][:], scalar=C.vec("l1_pool_scale", cc),
                               in1=sg[:], op0=ALU.mult, op1=ALU.mult),
                  reads=[("ps", pb), ("SA", cc % 2), "vecs"], writes=[("H", cc)])
    emit_proj_residual(C, "l1_pool_w_out", 16, C.H, lambda kc: [("H", kc)])


def emit_transpose_out(C, P, j, out_tm):
    S = C.S
    for tb in range(4):
        tmo = P["TMo"][tb % 2]
        for og in range(4):
            pb = og % 2
            for q in range(4):
                oc = og * 4 + q
                S.add("pe", _call("transpose", C.ps[pb][:, q * 128:(q + 1) * 128], C.X[:, oc, tb * 128:(tb + 1) * 128], C.ident[:]),
                      reads=[("X", oc), "ident"], writes=[("ps", pb)])
            if og % 2 == 0:
                S.add("dve", _call("tensor_copy", out=tmo[:, og * 512:(og + 1) * 512], in_=C.ps[pb][:]),
                      reads=[("ps", pb)], writes=[("TMo", tb % 2)])
            else:
                S.add("act", _call("activation", out=tmo[:, og * 512:(og + 1) * 512], in_=C.ps[pb][:], func=AF.Copy),
                      reads=[("ps", pb)], writes=[("TMo", tb % 2)])
        r0 = j * T + tb * 128
        S.add(C.store_eng, _call("dma_start", out=out_tm[r0:r0 + 128, :], in_=tmo[:]),
              reads=[("TMo", tb % 2)], writes=[("out", j, tb)], dsem="out")


def build_l3(nt=NT, stage=99):
    nc = bass.Bass("TRN2", target_bir_lowering=False)
    st = contextlib.ExitStack()
    with st:
        C = Ctx(nc, st, nt)
        S = C.S
        C.alloc_common()
        P = alloc_pool(C)
        x4T = C.din("x4T", [D, nt * T], F32)
        uT = C.din("uT", [D, nt * T], F32)
        sgT = C.din("sgT", [D, nt * T], F32)
        halo = C.din("halo", [nt, D, 16], F32)
        fix_in = C.din("fix", [128, nt * 4 * 16], F32)
        out_tm = C.dout("out", [nt * T, D], F32)
        S.add("sp", _call("dma_start", out=P["fix"][:], in_=fix_in), writes=["fix"], dsem="fix")
        C.cast_weight("l1_pool_w_grp", (D, 512))
        C.cast_weight("l1_pool_w_out", (D, D))
        if stage >= 2:
            C.cast_weight("l1_ffn2_w_in", (D, 2 * FF))
            C.cast_weight("l1_ffn2_w_out", (FF, D))
        for j in range(nt):
            C.load_X(x4T, j, ("x4T", j))
            emit_pool_mixer(C, P, j, uT, sgT, halo)
            C.layernorm("l1_ln2_g", "l1_ln2_b")
            if stage >= 2:
                C.ffn("l1_ffn2_w_in", "l1_ffn2_w_out")
                C.layernorm("l1_ln3_g", "l1_ln3_b")
            emit_transpose_out(C, P, j, out_tm)
        S.emit(final_waits=[("sp", "out")])
    return nc


def make_fix(gchunks):
    nt = len(gchunks)
    f = np.ones((nt, 4, 16), dtype=np.float32)
    for j, G in enumerate(gchunks):
        for g, w in enumerate(POOL_W):
            pos = G * T + np.arange(16)
            f[j, g] = w / np.minimum(pos + 1, w)
    return np.ascontiguousarray(np.broadcast_to(f.reshape(1, -1), (128, nt * 64)))


_NC_CACHE = {}


def _get_nc(name, builder):
    if name not in _NC_CACHE:
        _NC_CACHE[name] = builder()
    return _NC_CACHE[name]


def kernel_unfused(**inputs):
    inputs = {k: np.asarray(v) for k, v in inputs.items()}
    x = inputs["x"].astype(np.float32, copy=False)
    B, SEQ, _ = x.shape
    ncores = 8
    vecs = pack_vecs(inputs)
    cores = [(b, c) for b in range(B) for c in range(4)]
    xc = x.reshape(B, SEQ // T, T, D)

    nc1 = _get_nc("l1", build_l1)
    maps = []
    for (b, c) in cores:
        maps.append({"x_own": np.ascontiguousarray(xc[b, c::4].reshape(NT * T, D)), "vecs": vecs,
                     "l0_ffn1_w_in": inputs["l0_ffn1_w_in"], "l0_ffn1_w_out": inputs["l0_ffn1_w_out"],
                     "l0_sb_w_qkv": inputs["l0_sb_w_qkv"]})
    r1 = run_bass_kernel_spmd(nc1, maps, core_ids=list(range(ncores))).results

    nc2 = _get_nc("l2", build_l2)
    maps = []
    for i, (b, c) in enumerate(cores):
        kT_all = np.stack([np.asarray(r1[b * 4 + r]["kT"]) for r in range(4)])
        v_all = np.stack([np.asarray(r1[b * 4 + r]["v"]) for r in range(4)])
        m = {"x1T": np.asarray(r1[i]["x1T"]), "qT": np.asarray(r1[i]["qT"]), "kT_all": kT_all, "v_all": v_all,
             "masks": make_masks(c), "vecs": vecs}
        for n in ("l0_sb_w_o", "l0_ffn2_w_in", "l0_ffn2_w_out", "l1_ffn1_w_in", "l1_ffn1_w_out", "l1_pool_w_in"):
            m[n] = inputs[n]
        maps.append(m)
    r2 = run_bass_kernel_spmd(nc2, maps, core_ids=list(range(ncores))).results

    nc3 = _get_nc("l3", build_l3)
    maps = []
    wgrp = np.ascontiguousarray(inputs["l1_pool_w_grp"].reshape(D, 512))
    for i, (b, c) in enumerate(cores):
        halo = np.zeros((NT, D, 16), dtype=np.float32)
        for j in range(NT):
            G = 4 * j + c
            if G == 0:
                continue
            pc, pj = (G - 1) % 4, (G - 1) // 4
            halo[j] = np.asarray(r2[b * 4 + pc]["uT"])[:, (pj + 1) * T - 16:(pj + 1) * T]
        m = {"x4T": np.asarray(r2[i]["x4T"]), "uT": np.asarray(r2[i]["uT"]), "sgT": np.asarray(r2[i]["sgT"]),
             "halo": halo, "fix": make_fix([4 * j + c for j in range(NT)]), "vecs": vecs,
             "l1_pool_w_grp": wgrp, "l1_pool_w_out": inputs["l1_pool_w_out"],
             "l1_ffn2_w_in": inputs["l1_ffn2_w_in"], "l1_ffn2_w_out": inputs["l1_ffn2_w_out"]}
        maps.append(m)
    r3 = run_bass_kernel_spmd(nc3, maps, core_ids=list(range(ncores))).results

    out = np.empty((B, SEQ // T, T, D), dtype=np.float32)
    for i, (b, c) in enumerate(cores):
        out[b, c::4] = np.asarray(r3[i]["out"]).reshape(NT, T, D)
    return out.reshape(B, SEQ, D)


def build_fused(n_all=32, n_own=8, stageA=True):
    nq = n_own + 1
    nc = bass.Bass("TRN2", target_bir_lowering=False)
    st = contextlib.ExitStack()
    with st:
        C = Ctx(nc, st, nq)
        C.store_eng = "act"
        S = C.S
        C.alloc_common()
        x_loc = C.din("x_loc", [n_all * T, D], F32)
        flags_in = C.din("flags", [128, 64], F32)
        fix_in = C.din("fix", [128, n_own * 64], F32)
        dmask_in = C.din("dmask", [128, 4, T], BF16)
        out_tm = C.dout("out", [n_own * T, D], F32)
        x1T = C.dint("x1T_scr", [D, nq * T], F32)
        qT = C.dint("qT_scr", [HEADS, 128, nq * T], BF16)
        kT = C.dint("kT_scr", [HEADS, 128, n_all * T], BF16)
        vS = C.dint("v_scr", [HEADS, 128, n_all * 4, 128], BF16)
        x4T = C.dint("x4T_scr", [D, nq * T], F32)
        uT = C.dint("uT_scr", [D, nq * T], F32)
        sgT = C.dint("sgT_scr", [D, nq * T], F32)
        for name, shape in (("l0_ffn1_w_in", (D, 2 * FF)), ("l0_ffn1_w_out", (FF, D)), ("l0_sb_w_qkv", (D, 3072)),
                            ("l0_sb_w_o", (1024, D)), ("l0_ffn2_w_in", (D, 2 * FF)), ("l0_ffn2_w_out", (FF, D)),
                            ("l1_ffn1_w_in", (D, 2 * FF)), ("l1_ffn1_w_out", (FF, D)), ("l1_pool_w_in", (D, 2 * D)),
                            ("l1_pool_w_grp", (D, 512)), ("l1_pool_w_out", (D, D)),
                            ("l1_ffn2_w_in", (D, 2 * FF)), ("l1_ffn2_w_out", (FF, D))):
            C.cast_weight(name, shape, defer=(not name.startswith("l0_ffn1") and name != "l0_sb_w_qkv"))
        PH = C.sb("PH", [128, 12544], F32)
        flags = C.sb("flags_sb", [128, 64], F32)
        S.add("sp", _call("dma_start", out=flags[:], in_=flags_in), writes=["flags"], dsem="flags")
        A = {"gcount": 0}
        A["msk"] = C.sb("dmask_sb", [128, 4, T], BF16)
        S.add("sp", _call("dma_start", out=A["msk"][:], in_=dmask_in), writes=["msk"], dsem="msk")
        A["negtri"] = C.sb("negtri", [128, 128], BF16)
        A["negones"] = C.sb("negones", [128, 128], BF16)
        A["one"] = C.sb("one", [128, 1], F32)
        A["identb"] = C.sb("identb", [128, 128], BF16)
        S.add("pool", _call("memset", A["identb"][:], 0.0), writes=["identb"])
        S.add("pool", _call("affine_select", out=A["identb"][:], in_=A["identb"][:], pattern=[[-1, 128]],
                            compare_op=ALU.not_equal, fill=1.0, base=0, channel_multiplier=1),
              reads=["identb"], writes=["identb"])
        S.add("pool", _call("memset", A["negones"][:], -1.0), writes=["negones"])
        S.add("pool", _call("memset", A["one"][:], 1.0), writes=["one"])
        S.add("pool", _call("memset", A["negtri"][:], -1.0), writes=["negtri"])
        S.add("pool", _call("affine_select", out=A["negtri"][:], in_=A["negtri"][:], pattern=[[-1, 128]],
                            compare_op=ALU.is_ge, fill=0.0, base=0, channel_multiplier=1),
              reads=["negtri"], writes=["negtri"])
        hr = C.Hraw
        A["Kg"], A["Vg"] = [], []
        for i in range(3):
            kv = hr[:, i * 2048:(i + 1) * 2048]
            A["Kg"].append((kv, [("H", 4 * i + k) for k in range(4)]))
            vv = hr[:, 6144 + i * 2048:6144 + (i + 1) * 2048].rearrange("p (b d) -> p b d", d=128)
            A["Vg"].append((vv, [("H", 12 + 4 * i + k) for k in range(4)]))
        A["osb"] = (hr[:, 12288:16384].rearrange("p (h t) -> p h t", t=T), [("H", 24 + k) for k in range(8)])
        TM = [PH[:, 0:2048], PH[:, 2048:4096]]
        phb = PH[:, 4096:10240].bitcast(BF16)
        qsb = phb[:, 0:4096].rearrange("p (h t) -> p h t", t=T)
        ksb = phb[:, 4096:8192].rearrange("p (h t) -> p h t", t=T)
        vsb = phb[:, 8192:12288].rearrange("p (b n) -> p b n", n=1024)
        A["qsb"] = qsb
        A["E"] = [PH[:, k * 1024:(k + 1) * 1024] for k in range(2)]
        A["Lp"] = [PH[:, 2048 + k * 512:2048 + (k + 1) * 512].bitcast(BF16) for k in range(3)]
        A["W"] = [PH[:, 6144 + k * 512:6144 + (k + 1) * 512].bitcast(BF16) for k in range(3)]
        A["Lsb"] = [PH[:, 7680 + k * 256:7680 + (k + 1) * 256].bitcast(BF16) for k in range(3)]

        wq_v = C.wb["l0_sb_w_qkv"].rearrange("(kc p) n -> p kc n", p=128)
        for i in range(n_all if stageA else 0):
            for tb in range(4):
                tm = TM[tb % 2]
                r0 = i * T + tb * 128
                S.add("sp", _call("dma_start", out=tm, in_=x_loc[r0:r0 + 128, :]), writes=[("TM", tb % 2)], dsem=f"TM{tb % 2}")
                for og in range(4):
                    pb = og % 2
                    for q in range(4):
                        oc = og * 4 + q
                        S.add("pe", _call("transpose", C.ps[pb][:, q * 128:(q + 1) * 128], tm[:, oc * 128:(oc + 1) * 128], C.ident[:]),
                              reads=[("TM", tb % 2), "ident"], writes=[("ps", pb)])
                    pv = C.ps[pb][:].rearrange("p (q t) -> p q t", t=128)
                    S.add("dve", _call("tensor_copy", out=C.X[:, og * 4:(og + 1) * 4, tb * 128:(tb + 1) * 128], in_=pv),
                          reads=[("ps", pb)], writes=[("X", og * 4 + q) for q in range(4)])
                    S.add("act", _call("activation", out=C.Xb[:, og * 4:(og + 1) * 4, tb * 128:(tb + 1) * 128],
                                       in_=C.X[:, og * 4:(og + 1) * 4, tb * 128:(tb + 1) * 128], func=AF.Copy),
                          reads=[("X", og * 4 + q) for q in range(4)], writes=[("Xb", og * 4 + q) for q in range(4)])
            C.ffn("l0_ffn1_w_in", "l0_ffn1_w_out")
            wv = C.Hraw[:].rearrange("p (k n) -> p k n", n=1024)
            S.add("sp", _call("dma_start", out=wv, in_=wq_v[:, :, 2048:3072]),
                  reads=[("wb", "l0_sb_w_qkv")] + [("H", c) for c in range(32)], writes=[("H", c) for c in range(32)], dsem="wv")
            C.layernorm("l0_ln1_g", "l0_ln1_b")
            own = i < nq
            if own:
                C.store_X(x1T, i, ("x1T", i), dsem="scrA")
            for oc in (range(16) if own else range(8, 16)):
                si, slab = C.next_slab()
                sv = slab[:, 0:2048].rearrange("p (k n) -> p k n", n=128)
                S.add("sp", _call("dma_start", out=sv, in_=wq_v[:, :, oc * 128:(oc + 1) * 128]),
                      reads=[("wb", "l0_sb_w_qkv")], writes=[("slab", si)], dsem=f"slab{si}")
                pb = oc % 2
                for kc in range(NCH):
                    S.add("pe", _call("matmul", C.ps[pb][:], lhsT=sv[:, kc, :], rhs=C.Xb[:, kc, :], start=(kc == 0), stop=(kc == NCH - 1)),
                          reads=[("slab", si), ("Xb", kc)], writes=[("ps", pb)])
                if oc < 8:
                    S.add("act", _call("activation", out=qsb[:, oc, :], in_=C.ps[pb][:], func=AF.Copy, scale=SB_SCALE),
                          reads=[("ps", pb)], writes=["qsb"])
                else:
                    S.add("dve", _call("tensor_copy", out=ksb[:, oc - 8, :], in_=C.ps[pb][:]),
                          reads=[("ps", pb)], writes=["ksb"])
            if own:
                S.add("act", _call("dma_start", out=qT.rearrange("h d t -> d h t")[:, :, i * T:(i + 1) * T], in_=qsb),
                      reads=["qsb"], writes=[("qT", i)], dsem="scrA")
            S.add("act", _call("dma_start", out=kT.rearrange("h d t -> d h t")[:, :, i * T:(i + 1) * T], in_=ksb),
                  reads=["ksb"], writes=["kT"], dsem="scrA")
            for tb in range(4):
                for half in range(2):
                    pb = 2 + (tb * 2 + half) % 2
                    for kc in range(NCH):
                        S.add("pe", _call("matmul", C.ps[pb][:], lhsT=C.Xb[:, kc, tb * 128:(tb + 1) * 128],
                                          rhs=wv[:, kc, half * 512:(half + 1) * 512], start=(kc == 0), stop=(kc == NCH - 1)),
                              reads=[("H", c) for c in range(32)] + [("Xb", kc)], writes=[("ps", pb)])
                    S.add("dve", _call("tensor_scalar", out=vsb[:, tb, half * 512:(half + 1) * 512], in0=C.ps[pb][:],
                                       scalar1=flags[:, i:i + 1], scalar2=None, op0=ALU.mult),
                          reads=[("ps", pb), "flags"], writes=["vsb"])
            for tb in range(4):
                vdst = vS.rearrange("h p b d -> p b h d")[:, i * 4 + tb, :, :]
                S.add("act", _call("dma_start", out=vdst, in_=vsb[:, tb, :].rearrange("p (h d) -> p h d", d=128)),
                      reads=["vsb"], writes=["vS"], dsem="scrA")
        S.barrier()

        while C.deferred:
            C.emit_deferred_cast()
        for q in range(nq):
            S.add("sp", _call("dma_start", out=qsb, in_=qT.rearrange("h d t -> d h t")[:, :, q * T:(q + 1) * T]),
                  reads=[("qT", q)], writes=["qsb"], dsem="qsb")
            blocks = [(i, kb) for i in range(q, n_all) for kb in range(3, -1, -1)]
            n = len(blocks)
            for h in range(HEADS):
                po = 6
                grp_of = {}

                def load_group(i0, h=h, grp_of=grp_of):
                    gi = A["gcount"] % 3
                    A["gcount"] += 1
                    kg, kkeys = A["Kg"][gi]
                    vg, vkeys = A["Vg"][gi]
                    nchunk = min(4, n_all - i0)
                    S.add("sp", _call("dma_start", out=kg[:, 0:nchunk * T], in_=kT[h, :, i0 * T:(i0 + nchunk) * T]),
                          reads=["kT"], writes=kkeys, dsem=f"Kg{gi}")
                    S.add("sp", _call("dma_start", out=vg[:, 0:nchunk * 4, :], in_=vS[h, :, i0 * 4:(i0 + nchunk) * 4, :]),
                          reads=["vS"], writes=vkeys, dsem=f"Vg{gi}")
                    for ii in range(i0, i0 + nchunk):
                        grp_of[ii] = (kg, kkeys, vg, vkeys, ii - i0)

                def kslice(bi, grp_of=grp_of):
                    i, kb = blocks[bi]
                    if i not in grp_of:
                        load_group(i)
                    kg, kkeys, vg, vkeys, off = grp_of[i]
                    return (kg[:, off * T + kb * 128:off * T + (kb + 1) * 128], kkeys, vg[:, off * 4 + kb, :], vkeys)

                npair = n // 2
                psS2 = [C.ps2[0], C.ps2[1]]
                psW2 = C.ps2[2]
                SKEY = [[("ps", 0), ("ps", 1)], [("ps", 2), ("ps", 3)]]
                WKEY = [("ps", 4), ("ps", 5)]

                def stage_S(p, h=h):
                    for u in range(2):
                        ks, kkeys, _, _ = kslice(2 * p + u)
                        S.add("pe", _call("matmul", psS2[p % 2][:, u * T:(u + 1) * T], lhsT=ks, rhs=qsb[:, h, :], start=True, stop=True),
                              reads=kkeys + ["qsb"], writes=[SKEY[p % 2][u]])

                def stage_B(p):
                    i, kb = blocks[2 * p]
                    e = A["E"][p % 2]
                    lp = A["Lp"][p % 3]
                    S.add("act", _call("activation", out=e, in_=psS2[p % 2][:], func=AF.Exp), reads=SKEY[p % 2], writes=[("E", p % 2)])
                    S.add("act", _call("activation", out=lp, in_=e, func=AF.Ln, bias=A["one"][:, 0:1]),
                          reads=[("E", p % 2), "one"], writes=[("Lp", p % 3)])
                    if i == q:
                        r = 3 - kb
                        mk = A["msk"][:, r:r + 2, :].rearrange("p a t -> p (a t)")
                        S.add("dve", _call("tensor_tensor", out=lp, in0=lp, in1=mk, op=ALU.mult),
                              reads=[("Lp", p % 3), "msk"], writes=[("Lp", p % 3)])

                def stage_C(p, h=h):
                    lp = A["Lp"][p % 3]
                    for u in range(2):
                        ks, kkeys, _, _ = kslice(2 * p + u)
                        out = psW2[:, u * T:(u + 1) * T]
                        last = (p == 0 and u == 0)
                        S.add("pe", _call("matmul", out, lhsT=ks, rhs=qsb[:, h, :], start=True, stop=False),
                              reads=kkeys + ["qsb"], writes=[WKEY[u]])
                        S.add("pe", _call("matmul", out, lhsT=A["negtri"][:], rhs=lp[:, u * T:(u + 1) * T], start=False, stop=last),
                              reads=["negtri", ("Lp", p % 3)], writes=[WKEY[u]])
                        if u == 1:
                            S.add("pe", _call("matmul", out, lhsT=A["negones"][:], rhs=lp[:, 0:T], start=False, stop=(p == 0)),
                                  reads=["negones", ("Lp", p % 3)], writes=[WKEY[u]])
                        if p > 0:
                            S.add("pe", _call("matmul", out, lhsT=A["negones"][:], rhs=A["Lsb"][(p - 1) % 3], start=False, stop=True),
                                  reads=["negones", ("Lsb", (p - 1) % 3)], writes=[WKEY[u]])

                def stage_D(p):
                    i, kb = blocks[2 * p]
                    w = A["W"][p % 3]
                    S.add("act", _call("activation", out=w, in_=psW2[:], func=AF.Exp), reads=WKEY, writes=[("W", p % 3)])
                    if i == q:
                        r = 3 - kb
                        mk = A["msk"][:, r:r + 2, :].rearrange("p a t -> p (a t)")
                        S.add("dve", _call("scalar_tensor_tensor", out=w, in0=w, scalar=1.0, in1=mk,
                                           op0=ALU.min, op1=ALU.mult), reads=[("W", p % 3), "msk"], writes=[("W", p % 3)])
                    else:
                        S.add("dve", _call("tensor_scalar", out=w, in0=w, scalar1=1.0, scalar2=None, op0=ALU.min),
                              reads=[("W", p % 3)], writes=[("W", p % 3)])

                def stage_E(p):
                    if p == npair - 1:
                        return
                    lp = A["Lp"][p % 3]
                    for u in range(2):
                        S.add("pe", _call("matmul", C.ps[7][:], lhsT=A["identb"][:], rhs=lp[:, u * T:(u + 1) * T],
                                          start=(p == 0 and u == 0), stop=(p == npair - 2 and u == 1)),
                              reads=["identb", ("Lp", p % 3)], writes=[("ps", 7)])
                    S.add("dve", _call("tensor_copy", out=A["Lsb"][p % 3], in_=C.ps[7][:]), reads=[("ps", 7)], writes=[("Lsb", p % 3)])

                def stage_F(p):
                    w = A["W"][p % 3]
                    for u in range(2):
                        _, _, vs, vkeys = kslice(2 * p + u)
                        bi = 2 * p + u
                        S.add("pe", _call("matmul", C.ps[po][:], lhsT=vs, rhs=w[:, u * T:(u + 1) * T], start=(bi == 0), stop=(bi == n - 1)),
                              reads=vkeys + [("W", p % 3)], writes=[("ps", po)])

                for s_ in range(npair + 2):
                    if s_ < npair:
                        stage_S(s_)
                        stage_B(s_)
                    if 0 <= s_ - 1 < npair:
                        stage_C(s_ - 1)
                        stage_D(s_ - 1)
                        stage_E(s_ - 1)
                    if 0 <= s_ - 2 < npair:
                        stage_F(s_ - 2)
                osb, okeys = A["osb"]
                S.add("dve", _call("tensor_copy", out=osb[:, h, :], in_=C.ps[po][:]), reads=[("ps", po)], writes=okeys)
            C.load_X(x1T, q, ("x1T", q))
            osb, okeys = A["osb"]
            emit_proj_residual(C, "l0_sb_w_o", 8, osb, lambda kc: [okeys[kc]])
            C.layernorm("l0_ln2_g", "l0_ln2_b")
            C.ffn("l0_ffn2_w_in", "l0_ffn2_w_out")
            C.layernorm("l0_ln3_g", "l0_ln3_b")
            C.ffn("l1_ffn1_w_in", "l1_ffn1_w_out")
            C.layernorm("l1_ln1_g", "l1_ln1_b")
            C.store_X(x4T, q, ("x4T", q), dsem="scrB")
            wv2 = C.wb["l1_pool_w_in"].rearrange("(kc p) n -> p kc n", p=128)
            for oc in range(NCH):
                si, slab = C.next_slab()
                sv = slab[:].rearrange("p (k n) -> p k n", n=256)
                S.add("sp", _call("dma_start", out=sv[:, :, 0:128], in_=wv2[:, :, oc * 128:(oc + 1) * 128]),
                      reads=[("wb", "l1_pool_w_in")], writes=[("slab", si)], dsem=f"slab{si}")
                S.add("sp", _call("dma_start", out=sv[:, :, 128:256], in_=wv2[:, :, D + oc * 128:D + (oc + 1) * 128]),
                      reads=[("wb", "l1_pool_w_in")], writes=[("slab", si)], dsem=f"slab{si}")
                pu, pg = 2 * (oc % 2), 2 * (oc % 2) + 1
                for half, pb in ((0, pu), (1, pg)):
                    for kc in range(NCH):
                        S.add("pe", _call("matmul", C.ps[pb][:], lhsT=sv[:, kc, half * 128:(half + 1) * 128],
                                          rhs=C.Xb[:, kc, :], start=(kc == 0), stop=(kc == NCH - 1)),
                              reads=[("slab", si), ("Xb", kc)], writes=[("ps", pb)])
                su = C.SQ[oc % 2]
                sg = C.SA[oc % 2]
                S.add("dve", _call("tensor_copy", out=su[:], in_=C.ps[pu][:]), reads=[("ps", pu)], writes=[("SQ", oc % 2)])
                S.add("act", _call("activation", out=sg[:], in_=C.ps[pg][:], func=AF.Silu), reads=[("ps", pg)], writes=[("SA", oc % 2)])
                S.add("act", _call("dma_start", out=uT[oc * 128:(oc + 1) * 128, q * T:(q + 1) * T], in_=su[:]),
                      reads=[("SQ", oc % 2)], writes=[("uT", q)], dsem="scrB")
                S.add("act", _call("dma_start", out=sgT[oc * 128:(oc + 1) * 128, q * T:(q + 1) * T], in_=sg[:]),
                      reads=[("SA", oc % 2)], writes=[("sgT", q)], dsem="scrB")
        S.barrier()

        P = {}
        uw = 4 * (T + 16)
        P["UE"] = [PH[:, k * uw:(k + 1) * uw].rearrange("p (c t) -> p c t", t=T + 16) for k in range(2)]
        P["B"] = [PH[:, (2 + k) * uw:(3 + k) * uw].rearrange("p (c t) -> p c t", t=T + 16) for k in range(2)]
        P["TMo"] = [PH[:, 4 * uw + k * 2048:4 * uw + (k + 1) * 2048] for k in range(2)]
        P["fix"] = C.sb("fixsb", [128, n_own * 64], F32)
        S.add("sp", _call("dma_start", out=P["fix"][:], in_=fix_in), writes=["fix"], dsem="fix")
        for q in range(n_own):
            C.load_X(x4T, q, ("x4T", q))
            emit_pool_mixer_f(C, P, q, uT, sgT, flags)
            C.layernorm("l1_ln2_g", "l1_ln2_b")
            C.ffn("l1_ffn2_w_in", "l1_ffn2_w_out")
            C.layernorm("l1_ln3_g", "l1_ln3_b")
            emit_transpose_out(C, P, q, out_tm)
        S.emit(final_waits=[("act", "out")])
    return nc


def emit_pool_mixer_f(C, P, q, uT, sgT, flags):
    S = C.S
    uv = uT.rearrange("(c p) t -> p c t", p=128)
    fixv = P["fix"][:].rearrange("p (j g t) -> p j g t", g=4, t=16)
    wgv = C.wb["l1_pool_w_grp"].rearrange("(kc p) n -> p kc n", p=128)
    for g in range(4):
        w = POOL_W[g]
        ue = P["UE"][g % 2]
        uk = ("UE", g % 2)
        S.add("sp", _call("dma_start", out=ue[:, :, 16:16 + T], in_=uv[:, 4 * g:4 * g + 4, q * T:(q + 1) * T]),
              reads=[("uT", q)], writes=[uk], dsem=f"UE{g % 2}")
        S.add("sp", _call("dma_start", out=ue[:, :, 0:16], in_=uv[:, 4 * g:4 * g + 4, (q + 2) * T - 16:(q + 2) * T]),
              reads=[("uT", q + 1)], writes=[uk], dsem=f"UE{g % 2}")
        S.add("pool", _call("tensor_scalar", out=ue[:, :, 0:16], in0=ue[:, :, 0:16], scalar1=flags[:, q + 1:q + 2], scalar2=None, op0=ALU.mult),
              reads=[uk, "flags"], writes=[uk])
        src, sk = ue, uk
        sh = 1
        for lvl in range(g + 1):
            dst, dk = P["B"][lvl % 2], ("PB", lvl % 2)
            lo = 2 * sh - 1
            S.add("pool", _call("tensor_tensor", out=dst[:, :, lo:T + 16], in0=src[:, :, lo:T + 16], in1=src[:, :, lo - sh:T + 16 - sh], op=ALU.add),
                  reads=[sk], writes=[dk])
            src, sk = dst, dk
            sh *= 2
        for c in range(4):
            S.add("pool", _call("tensor_tensor", out=src[:, c, 16:32], in0=src[:, c, 16:32], in1=fixv[:, q, g, :], op=ALU.mult),
                  reads=[sk, "fix"], writes=[sk])
        S.add("dve", _call("scalar_tensor_tensor", out=C.Xb[:, 4 * g:4 * g + 4, :], in0=src[:, :, 16:16 + T], scalar=1.0 / w,
                           in1=ue[:, :, 16:16 + T], op0=ALU.mult, op1=ALU.subtract),
              reads=[sk, uk], writes=[("Xb", 4 * g + c) for c in range(4)])
        si, slab = C.next_slab()
        sv = slab[:, 0:2048].rearrange("p (k n) -> p k n", n=512)
        S.add("sp", _call("dma_start", out=sv, in_=wgv[:, 4 * g:4 * g + 4, :]),
              reads=[("wb", "l1_pool_w_grp")], writes=[("slab", si)], dsem=f"slab{si}")
        for oc in range(4):
            cc = 4 * g + oc
            pb = cc % 2
            for kc in range(4):
                S.add("pe", _call("matmul", C.ps[pb][:], lhsT=sv[:, kc, oc * 128:(oc + 1) * 128], rhs=C.Xb[:, 4 * g + kc, :],
                                  start=(kc == 0), stop=(kc == 3)),
                      reads=[("slab", si), ("Xb", 4 * g + kc)], writes=[("ps", pb)])
            sg = C.SA[cc % 2]
            S.add("sp", _call("dma_start", out=sg[:], in_=sgT[cc * 128:(cc + 1) * 128, q * T:(q + 1) * T]),
                  reads=[("sgT", q)], writes=[("SA", cc % 2)], dsem=f"SA{cc % 2}")
            S.add("dve", _call("scalar_tensor_tensor", out=C.H[:, cc, :], in0=C.ps[pb][:], scalar=C.vec("l1_pool_scale", cc),
                               in1=sg[:], op0=ALU.mult, op1=ALU.mult),
                  reads=[("ps", pb), ("SA", cc % 2), "vecs"], writes=[("H", cc)])
    emit_proj_residual(C, "l1_pool_w_out", 16, C.H, lambda kc: [("H", kc)])


def make_dmask():
    s = np.arange(128)[:, None, None]
    kb = (3 - np.arange(4))[None, :, None]
    t = np.arange(T)[None, None, :]
    return np.ascontiguousarray(((kb * 128 + s) < t).astype(np.float32).astype(ml_dtypes.bfloat16))


def kernel(**inputs):
    inputs = {k: np.asarray(v) for k, v in inputs.items()}
    x = inputs["x"].astype(np.float32, copy=False)
    B, SEQ, _ = x.shape
    nchunks = SEQ // T
    n_all, n_own = nchunks, nchunks // 4
    vecs = pack_vecs(inputs)
    dmask = make_dmask()
    cores = [(b, c) for b in range(B) for c in range(4)]
    nc = _get_nc("fused", lambda: build_fused(n_all, n_own))
    wnames = ("l0_ffn1_w_in", "l0_ffn1_w_out", "l0_sb_w_qkv", "l0_sb_w_o", "l0_ffn2_w_in", "l0_ffn2_w_out",
              "l1_ffn1_w_in", "l1_ffn1_w_out", "l1_pool_w_in", "l1_pool_w_out", "l1_ffn2_w_in", "l1_ffn2_w_out")
    wgrp = np.ascontiguousarray(inputs["l1_pool_w_grp"].reshape(D, 512))
    maps = []
    for (b, c) in cores:
        newest = n_own * c + n_own - 1
        x_loc = np.zeros((n_all * T, D), dtype=np.float32)
        flags = np.zeros((128, 64), dtype=np.float32)
        for i in range(n_all):
            G = newest - i
            if G >= 0:
                x_loc[i * T:(i + 1) * T] = x[b, G * T:(G + 1) * T]
                flags[:, i] = 1.0
        m = {"x_loc": x_loc, "flags": flags, "fix": make_fix([newest - q for q in range(n_own)]),
             "dmask": dmask, "vecs": vecs, "l1_pool_w_grp": wgrp}
        for n in wnames:
            m[n] = inputs[n]
        maps.append(m)
    res = run_bass_kernel_spmd(nc, maps, core_ids=list(range(len(cores)))).results
    out = np.empty((B, SEQ, D), dtype=np.float32)
    for i, (b, c) in enumerate(cores):
        o = np.asarray(res[i]["out"])
        newest = n_own * c + n_own - 1
        for q in range(n_own):
            G = newest - q
            out[b, G * T:(G + 1) * T] = o[q * T:(q + 1) * T]
    return out
```
